# Optimizing a Trainium2 kernel written in Bass

```python
import math
import jax, jax.numpy as jnp
from jax import lax
import numpy as np

D_MODEL = 1024
BATCH = 2
SEQ = 8192
DEPTH = 2

N_META = 16
CHUNK = 64
PAD = CHUNK - N_META
CONV_K = 4
RET_HEADS = 4
RET_DK = D_MODEL // 8
RET_DV = D_MODEL // 4
RET_QK = RET_HEADS * RET_DK
RET_VW = RET_HEADS * RET_DV
ROPE_BASE = 10000.0
GDN_HEADS = 4
GDN_DK = D_MODEL // 8
GDN_DV = D_MODEL // 4
GDN_QK = GDN_HEADS * GDN_DK
GDN_VW = GDN_HEADS * GDN_DV
AB_SIZES = (RET_QK, RET_QK, RET_VW, RET_VW, GDN_QK, GDN_QK, GDN_VW, GDN_HEADS, GDN_HEADS, GDN_VW)
AB_IN = 2 * RET_QK + 2 * RET_VW + 2 * GDN_QK + 2 * GDN_VW + 2 * GDN_HEADS
AB_OUT = RET_VW + GDN_VW
SSD_DINNER = 2 * D_MODEL
SSD_HEADDIM = 64
SSD_HEADS = SSD_DINNER // SSD_HEADDIM
SSD_GROUPS = 4
SSD_HPG = SSD_HEADS // SSD_GROUPS
SSD_DSTATE = 128
SSD_GN = SSD_GROUPS * SSD_DSTATE
SSD_CONV_CH = SSD_DINNER + 2 * SSD_GN
SSD_IN = SSD_DINNER + SSD_CONV_CH + SSD_HEADS
D_FF = 4 * D_MODEL
DN_ALPHA = (2 * DEPTH) ** 0.25
DN_BETA = (8 * DEPTH) ** -0.25
N_AB = (DEPTH + 1) // 2
N_C = DEPTH // 2
LN_EPS = 1e-5
GN_EPS = 1e-5
RMS_EPS = 1e-6

kernel_name = 'hybrid_retnet_gdn_mamba2_deepnorm'


def split_cols(a, sizes):
    offs, acc = [], 0
    for s in sizes[:-1]:
        acc += s
        offs.append(acc)
    return jnp.split(a, offs, axis=-1)


def _standardize(x, eps):
    xf = x.astype(jnp.float32)
    xc = xf - jnp.mean(xf, -1, keepdims=True)
    return xc * lax.rsqrt(jnp.mean(xc * xc, -1, keepdims=True) + eps)


def _rms(x, eps):
    xf = x.astype(jnp.float32)
    return xf * lax.rsqrt(jnp.mean(xf * xf, -1, keepdims=True) + eps)


def layer_norm(x, w, b):
    return (_standardize(x, LN_EPS) * w + b).astype(x.dtype)


def l2norm(x):
    xf = x.astype(jnp.float32)
    return xf * lax.rsqrt(jnp.sum(xf * xf, -1, keepdims=True) + 1e-6)


def rope(x, cos, sin):
    x1, x2 = jnp.split(x.astype(jnp.float32), 2, axis=-1)
    return jnp.concatenate([x1 * cos - x2 * sin, x1 * sin + x2 * cos], axis=-1)


def causal_dwconv(x, w):
    L = x.shape[1]
    xp = jnp.pad(x, ((0, 0), (CONV_K - 1, 0), (0, 0)))
    return sum(xp[:, k:k + L] * w[k] for k in range(CONV_K))


def to_chunks(a):
    a = jnp.pad(a, [(0, 0), (PAD, 0)] + [(0, 0)] * (a.ndim - 2))
    nc = a.shape[1] // CHUNK
    a = a.reshape((a.shape[0], nc, CHUNK) + a.shape[2:])
    return jnp.moveaxis(a, 1, 0)


def from_chunks(o):
    o = jnp.moveaxis(o, 0, 1)
    o = o.reshape((o.shape[0], -1) + o.shape[3:])
    return o[:, PAD:]


def chunked_decay_attention(q, k, v, g):
    B, _, G, dk = q.shape
    Hg, dv = v.shape[-2:]
    xs = tuple(to_chunks(a.astype(jnp.float32)) for a in (q, k, v, g))
    incl = jnp.tril(jnp.ones((CHUNK, CHUNK), bool))[None, :, :, None, None]

    def body(S, inp):
        qc, kc, vc, gc = inp
        gcum = jnp.cumsum(gc, axis=1)
        seg = gcum[:, :, None] - gcum[:, None]
        att = jnp.einsum('btgd,bsgd->btsg', qc, kc)[..., None] * jnp.exp(jnp.where(incl, seg, -jnp.inf))
        o = jnp.einsum('btsgh,bsghv->btghv', att, vc)
        o = o + jnp.einsum('btgd,bghdv->btghv', qc, S) * jnp.exp(gcum)[..., None]
        g_last = gcum[:, -1]
        S = S * jnp.exp(g_last)[..., None, None] + jnp.einsum(
            'bsgd,bsghv->bghdv', kc, vc * jnp.exp(g_last[:, None] - gcum)[..., None])
        return S, o

    S0 = jnp.zeros((B, G, Hg, dk, dv), jnp.float32)
    _, o = lax.scan(body, S0, xs)
    return from_chunks(o)


def chunked_gated_delta_rule(q, k, v, beta, g):
    B, _, H, dk = q.shape
    dv = v.shape[-1]
    xs = tuple(to_chunks(a.astype(jnp.float32)) for a in (q, k, v, beta, g))
    eye = jnp.eye(CHUNK, dtype=jnp.float32)
    strict = jnp.tril(jnp.ones((CHUNK, CHUNK), bool), -1)
    incl = jnp.tril(jnp.ones((CHUNK, CHUNK), bool))

    def body(S, inp):
        qc, kc, vc, bc, gc = (jnp.moveaxis(a, 1, 2) for a in inp)
        gcum = jnp.cumsum(gc, axis=-1)
        seg = gcum[..., :, None] - gcum[..., None, :]
        kb = kc * bc[..., None]
        A = jnp.einsum('bhtd,bhsd->bhts', kb, kc) * jnp.exp(jnp.where(strict, seg, -jnp.inf))
        rhs = jnp.concatenate([vc * bc[..., None], kb * jnp.exp(gcum)[..., None]], axis=-1)
        u = lax.linalg.triangular_solve(eye + A, rhs, left_side=True, lower=True)
        w_v, w_k = u[..., :dv], u[..., dv:]
        v_new = w_v - jnp.einsum('bhcd,bhdv->bhcv', w_k, S)
        att = jnp.einsum('bhtd,bhsd->bhts', qc, kc) * jnp.exp(jnp.where(incl, seg, -jnp.inf))
        o = jnp.einsum('bhtd,bhdv->bhtv', qc * jnp.exp(gcum)[..., None], S) + jnp.einsum('bhts,bhsv->bhtv', att, v_new)
        g_last = gcum[..., -1:]
        S = S * jnp.exp(g_last)[..., None] + jnp.einsum(
            'bhsd,bhsv->bhdv', kc * jnp.exp(g_last - gcum)[..., None], v_new)
        return S, jnp.moveaxis(o, 1, 2)

    S0 = jnp.zeros((B, H, dk, dv), jnp.float32)
    _, o = lax.scan(body, S0, xs)
    return from_chunks(o)


def retention_gdn_mixer(h, w_in, ret_gn_w, conv_q, conv_k, conv_v, A_log, dt_bias, gdn_norm_w, w_out, cos, sin):
    B, L, _ = h.shape
    rq, rk, rv, rg, gq, gk, gv, gb, ga, gg = split_cols(h @ w_in, AB_SIZES)
    rq = rope(rq.reshape(B, L, RET_HEADS, RET_DK), cos, sin) * RET_DK ** -0.5
    rk = rope(rk.reshape(B, L, RET_HEADS, RET_DK), cos, sin)
    rv = rv.reshape(B, L, RET_HEADS, 1, RET_DV)
    log_gamma = jnp.log1p(-jnp.exp2(-5.0 - jnp.arange(RET_HEADS, dtype=jnp.float32)))
    rgam = jnp.broadcast_to(log_gamma[:, None], (B, L, RET_HEADS, 1))
    ro = chunked_decay_attention(rq, rk, rv, rgam).reshape(B, L, RET_HEADS, RET_DV)
    ro = _standardize(ro, GN_EPS).reshape(B, L, RET_VW) * ret_gn_w
    ret_out = jax.nn.silu(rg.astype(jnp.float32)) * ro
    gq = l2norm(jax.nn.silu(causal_dwconv(gq, conv_q)).reshape(B, L, GDN_HEADS, GDN_DK)) * GDN_DK ** -0.5
    gk = l2norm(jax.nn.silu(causal_dwconv(gk, conv_k)).reshape(B, L, GDN_HEADS, GDN_DK))
    gv = jax.nn.silu(causal_dwconv(gv, conv_v)).reshape(B, L, GDN_HEADS, GDN_DV)
    beta = jax.nn.sigmoid(gb.astype(jnp.float32))
    gdec = -jnp.exp(A_log.astype(jnp.float32)) * jax.nn.softplus(ga.astype(jnp.float32) + dt_bias)
    go = chunked_gated_delta_rule(gq, gk, gv, beta, gdec)
    go = (_rms(go, RMS_EPS) * gdn_norm_w).reshape(B, L, GDN_VW) * jax.nn.silu(gg.astype(jnp.float32))
    y = jnp.concatenate([ret_out, go], axis=-1).astype(h.dtype)
    return y @ w_out


def mamba2_mixer(h, w_in, conv_w, conv_b, A_log, dt_bias, D, norm_w, w_out):
    B, L, _ = h.shape
    z, xbc, dt = split_cols(h @ w_in, (SSD_DINNER, SSD_CONV_CH, SSD_HEADS))
    xbc = jax.nn.silu(causal_dwconv(xbc, conv_w) + conv_b)
    xs, Bm, Cm = split_cols(xbc, (SSD_DINNER, SSD_GN, SSD_GN))
    xs = xs.astype(jnp.float32).reshape(B, L, SSD_GROUPS, SSD_HPG, SSD_HEADDIM)
    Bm = Bm.reshape(B, L, SSD_GROUPS, SSD_DSTATE)
    Cm = Cm.reshape(B, L, SSD_GROUPS, SSD_DSTATE)
    dt = jax.nn.softplus(dt.astype(jnp.float32) + dt_bias).reshape(B, L, SSD_GROUPS, SSD_HPG)
    A = -jnp.exp(A_log.astype(jnp.float32)).reshape(SSD_GROUPS, SSD_HPG)
    y = chunked_decay_attention(Cm, Bm, xs * dt[..., None], dt * A)
    y = y + xs * D.reshape(SSD_GROUPS, SSD_HPG, 1)
    y = y.reshape(B, L, SSD_DINNER) * jax.nn.silu(z.astype(jnp.float32))
    y = _rms(y.reshape(B, L, SSD_GROUPS, SSD_DINNER // SSD_GROUPS), RMS_EPS).reshape(B, L, SSD_DINNER) * norm_w
    return y.astype(h.dtype) @ w_out


def sq_relu_mlp(h, w1, w2):
    return jnp.square(jax.nn.relu(h @ w1)) @ w2


def setup_inputs(seed: int = 0) -> dict:
    key = jax.random.key(seed)
    ks = iter(jax.random.split(key, 32))

    def nrm(shape, scale):
        return jax.random.normal(next(ks), shape, jnp.float32) * scale

    def gain(shape):
        return 1.0 + nrm(shape, 0.02)

    def dt_bias(shape):
        dt = jnp.exp(jax.random.uniform(next(ks), shape, jnp.float32, math.log(1e-3), math.log(1e-1)))
        return dt + jnp.log(-jnp.expm1(-dt))

    def a_log(shape):
        return jnp.log(jax.random.uniform(next(ks), shape, jnp.float32, 1.0, 16.0))

    return {
        'x': nrm((BATCH, SEQ, D_MODEL), 1.0),
        'meta_tokens': nrm((N_META, D_MODEL), 1.0),
        'ab_w_in': nrm((N_AB, D_MODEL, AB_IN), D_MODEL ** -0.5),
        'ab_ret_gn_w': gain((N_AB, RET_VW)),
        'ab_conv_q': nrm((N_AB, CONV_K, GDN_QK), CONV_K ** -0.5),
        'ab_conv_k': nrm((N_AB, CONV_K, GDN_QK), CONV_K ** -0.5),
        'ab_conv_v': nrm((N_AB, CONV_K, GDN_VW), CONV_K ** -0.5),
        'ab_A_log': a_log((N_AB, GDN_HEADS)),
        'ab_dt_bias': dt_bias((N_AB, GDN_HEADS)),
        'ab_gdn_norm_w': gain((N_AB, GDN_DV)),
        'ab_w_out': nrm((N_AB, AB_OUT, D_MODEL), AB_OUT ** -0.5 * DN_BETA),
        'c_w_in': nrm((N_C, D_MODEL, SSD_IN), D_MODEL ** -0.5),
        'c_conv_w': nrm((N_C, CONV_K, SSD_CONV_CH), CONV_K ** -0.5),
        'c_conv_b': nrm((N_C, SSD_CONV_CH), 0.02),
        'c_A_log': a_log((N_C, SSD_HEADS)),
        'c_dt_bias': dt_bias((N_C, SSD_HEADS)),
        'c_D': gain((N_C, SSD_HEADS)),
        'c_norm_w': gain((N_C, SSD_DINNER)),
        'c_w_out': nrm((N_C, SSD_DINNER, D_MODEL), SSD_DINNER ** -0.5 * DN_BETA),
        'mlp_w1': nrm((DEPTH, D_MODEL, D_FF), D_MODEL ** -0.5),
        'mlp_w2': nrm((DEPTH, D_FF, D_MODEL), D_FF ** -0.5 * DN_BETA),
        'ln1_w': gain((DEPTH, D_MODEL)),
        'ln1_b': nrm((DEPTH, D_MODEL), 0.02),
        'ln2_w': gain((DEPTH, D_MODEL)),
        'ln2_b': nrm((DEPTH, D_MODEL), 0.02),
    }


def reference(x, meta_tokens, ab_w_in, ab_ret_gn_w, ab_conv_q, ab_conv_k, ab_conv_v, ab_A_log, ab_dt_bias,
              ab_gdn_norm_w, ab_w_out, c_w_in, c_conv_w, c_conv_b, c_A_log, c_dt_bias, c_D, c_norm_w, c_w_out,
              mlp_w1, mlp_w2, ln1_w, ln1_b, ln2_w, ln2_b):
    B = x.shape[0]
    meta = jnp.broadcast_to(meta_tokens[None].astype(x.dtype), (B, N_META, D_MODEL))
    h = jnp.concatenate([meta, x], axis=1)
    L = h.shape[1]
    pos = jnp.arange(L, dtype=jnp.float32)
    inv_freq = 1.0 / (ROPE_BASE ** jnp.linspace(0.0, 1.0, RET_DK // 2, dtype=jnp.float32))
    ang = pos[:, None] * inv_freq[None]
    cos = jnp.cos(ang)[None, :, None, :]
    sin = jnp.sin(ang)[None, :, None, :]
    for i in range(DEPTH):
        j = i // 2
        if i % 2 == 0:
            mix = retention_gdn_mixer(h, ab_w_in[j], ab_ret_gn_w[j], ab_conv_q[j], ab_conv_k[j], ab_conv_v[j],
                                      ab_A_log[j], ab_dt_bias[j], ab_gdn_norm_w[j], ab_w_out[j], cos, sin)
        else:
            mix = mamba2_mixer(h, c_w_in[j], c_conv_w[j], c_conv_b[j], c_A_log[j], c_dt_bias[j], c_D[j],
                               c_norm_w[j], c_w_out[j])
        h = layer_norm(DN_ALPHA * h + mix, ln1_w[i], ln1_b[i])
        h = layer_norm(DN_ALPHA * h + sq_relu_mlp(h, mlp_w1[i], mlp_w2[i]), ln2_w[i], ln2_b[i])
    return h[:, N_META:]
```

```python
import numpy as np
import concourse.bass as bass
import concourse.mybir as mybir
from concourse.bass_utils import run_bass_kernel_spmd

F32 = mybir.dt.float32
BF16 = mybir.dt.bfloat16
AF = mybir.ActivationFunctionType
ALU = mybir.AluOpType

D = 1024
KC = 8
NB = 2
SEQ = 8192
NMETA = 16
PADF = 48
R = 8320
NT = R // 128
TQ = R // 4
NTOK = 208
DFF = 4096
FC = DFF // 128
ALPHA = float((2 * 2) ** 0.25)
LN_EPS = 1e-5
DEBUG_STAGE = 99


class FW:
    def __init__(self, nc):
        self.nc = nc
        self.eng = {}
        for n in ["tensor", "vector", "scalar", "gpsimd", "sync"]:
            self.eng[n] = dict(h=getattr(nc, n), sem=nc.alloc_semaphore("s_" + n), cnt=0, seen={})
        self.lastw = {}
        self.readers = {}
        self.dma_sems = {}
        self.nins = 0
        self.psum = set()

    def _need(self, e, tok):
        if tok is None:
            return
        sem, val, src = tok
        E = self.eng[e]
        if src == e and e == "tensor":
            return
        k = id(sem)
        if E["seen"].get(k, 0) >= val:
            return
        E["h"].wait_ge(sem, val)
        E["seen"][k] = val

    def _deps(self, e, reads, writes):
        for k in reads:
            self._need(e, self.lastw.get(k))
            if k in self.psum:
                for r in self.readers.get(k, []):
                    if r[2] != e:
                        self._need(e, r)
        for k in writes:
            self._need(e, self.lastw.get(k))
            for r in self.readers.get(k, []):
                self._need(e, r)

    def _mark(self, tok, reads, writes):
        for k in writes:
            self.lastw[k] = tok
            self.readers[k] = []
        for k in reads:
            if k in writes:
                continue
            self.readers.setdefault(k, []).append(tok)

    def op(self, e, fn, reads=(), writes=()):
        self._deps(e, reads, writes)
        E = self.eng[e]
        ins = fn(E["h"])
        E["cnt"] += 1
        ins.then_inc(E["sem"], 1)
        tok = (E["sem"], E["cnt"], e)
        self._mark(tok, reads, writes)
        self.nins += 1
        return tok

    def dma(self, e, out, in_, reads=(), writes=(), semkey=None, **kw):
        self._deps(e, reads, writes)
        sk = semkey if semkey is not None else (writes[0] if writes else reads[0])
        if sk not in self.dma_sems:
            self.dma_sems[sk] = [self.nc.alloc_semaphore("d_%d" % len(self.dma_sems)), 0]
        S = self.dma_sems[sk]
        if S[1] > 0:
            self._need(e, (S[0], S[1], "dma"))
        E = self.eng[e]
        ins = E["h"].dma_start(out=out, in_=in_, **kw)
        S[1] += 16
        ins.then_inc(S[0], 16)
        tok = (S[0], S[1], "dma")
        self._mark(tok, reads, writes)
        self.nins += 1
        return tok

    def finish(self, keys):
        for k in keys:
            self._need("sync", self.lastw.get(k))


class Rot:
    def __init__(self, items):
        self.items = items
        self.i = 0

    def next(self):
        it = self.items[self.i % len(self.items)]
        self.i += 1
        return it


def make_consts(nc, fw):
    c = {}
    ident = nc.alloc_sbuf_tensor("ident", [128, 128], F32)
    fw.op("gpsimd", lambda e: e.memset(ident[:], 1.0), writes=["ident"])
    fw.op("gpsimd", lambda e: e.affine_select(out=ident[:], in_=ident[:], pattern=[[-1, 128]], compare_op=ALU.is_equal,
                                              fill=0.0, base=0, channel_multiplier=1), reads=["ident"], writes=["ident"])
    c["ident"] = ident
    return c


def emit_mlp_phase(nc, fw, res, parts, w1, w2, lnp, out, pfx="m"):
    N = NTOK
    ntile = TQ // N
    W1 = nc.alloc_sbuf_tensor(pfx + "W1", [128, KC, DFF], BF16)
    W2 = nc.alloc_sbuf_tensor(pfx + "W2", [128, FC, D], BF16)
    LNP = nc.alloc_sbuf_tensor(pfx + "LNP", [128, 4, KC], F32)
    onesS = nc.alloc_sbuf_tensor(pfx + "onesS", [128, 128], F32)
    epsT = nc.alloc_sbuf_tensor(pfx + "eps", [128, 1], F32)
    fw.op("gpsimd", lambda e: e.memset(onesS[:], 1.0 / D), writes=["onesS"])
    fw.op("gpsimd", lambda e: e.memset(epsT[:], LN_EPS), writes=["epsT"])
    for j in range(4):
        fw.dma("sync", LNP[:, j, :], lnp[j].rearrange("(kc p) -> p kc", p=128), writes=["LNP"], semkey="ldc",
               allow_slow_non_contiguous=True)
    w1v = w1.rearrange("(kc p) f -> p kc f", p=128)
    for kc in range(KC):
        for hf in range(2):
            fw.dma("gpsimd", W1[:, kc, hf * 2048:(hf + 1) * 2048], w1v[:, kc, hf * 2048:(hf + 1) * 2048],
                   writes=["W1"], semkey="ldw%d" % ((kc * 2 + hf) % 4))
    w2v = w2.rearrange("(fc p) o -> p fc o", p=128)
    for fc in range(0, FC, 2):
        fw.dma("gpsimd", W2[:, fc:fc + 2, :], w2v[:, fc:fc + 2, :], writes=["W2"], semkey="ldw%d" % ((fc // 2) % 4))

    ld = Rot([(nc.alloc_sbuf_tensor(pfx + "ld%d" % i, [128, KC, N], F32), pfx + "ld%d" % i) for i in range(2)])
    S = nc.alloc_sbuf_tensor(pfx + "S", [128, KC, N], F32)
    SQ = nc.alloc_sbuf_tensor(pfx + "SQ", [128, KC, N], F32)
    HA = nc.alloc_sbuf_tensor(pfx + "HA", [128, KC, N], F32)
    HAB = nc.alloc_sbuf_tensor(pfx + "HAB", [128, KC, N], BF16)
    HID = nc.alloc_sbuf_tensor(pfx + "HID", [128, FC, N], BF16)
    RL = Rot([(nc.alloc_sbuf_tensor(pfx + "RL%d" % i, [128, N], F32), pfx + "RL%d" % i) for i in range(3)])
    MS = nc.alloc_sbuf_tensor(pfx + "MS", [128, N], F32)
    M2 = nc.alloc_sbuf_tensor(pfx + "M2", [128, N], F32)
    RS = nc.alloc_sbuf_tensor(pfx + "RS", [128, N], F32)
    OUT = nc.alloc_sbuf_tensor(pfx + "OUT", [128, KC, N], F32)
    psm = nc.alloc_psum_tensor(pfx + "psm", [128, 512], F32)
    psq = nc.alloc_psum_tensor(pfx + "psq", [128, 512], F32)
    psu = Rot([(nc.alloc_psum_tensor(pfx + "psu%d" % i, [128, 512], F32), pfx + "psu%d" % i) for i in range(3)])
    psd = Rot([(nc.alloc_psum_tensor(pfx + "psd%d" % i, [128, 512], F32), pfx + "psd%d" % i) for i in range(2)])
    resv = res.rearrange("(kc p) t -> p kc t", p=128)
    partv = [p_.rearrange("(kc p) t -> p kc t", p=128) for p_ in parts]
    outv = out.rearrange("(kc p) t -> p kc t", p=128)
    fw.psum.update(["psm", "psq"] + [k for _, k in psu.items] + [k for _, k in psd.items])

    def layer_norm(src, skey, dst, dkey, jw, dstb=None, dbkey=None):
        fw.op("scalar", lambda e: e.activation(out=SQ[:], in_=src[:], func=AF.Square), reads=[skey], writes=["SQ"])
        for kc in range(KC):
            fw.op("tensor", lambda e: e.matmul(psm[:, 0:N], lhsT=onesS[:], rhs=src[:, kc, :], start=(kc == 0), stop=(kc == KC - 1)),
                  reads=[skey, "onesS"], writes=["psm"])
        for kc in range(KC):
            fw.op("tensor", lambda e: e.matmul(psq[:, 0:N], lhsT=onesS[:], rhs=SQ[:, kc, :], start=(kc == 0), stop=(kc == KC - 1)),
                  reads=["SQ", "onesS"], writes=["psq"])
        fw.op("scalar", lambda e: e.activation(out=MS[:], in_=psm[:, 0:N], func=AF.Copy), reads=["psm"], writes=["MS"])
        fw.op("gpsimd", lambda e: e.tensor_tensor(out=M2[:], in0=MS[:], in1=MS[:], op=ALU.mult), reads=["MS"], writes=["M2"])
        fw.op("vector", lambda e: e.tensor_tensor(out=RS[:], in0=psq[:, 0:N], in1=M2[:], op=ALU.subtract), reads=["psq", "M2"], writes=["RS"])
        fw.op("scalar", lambda e: e.activation(out=RS[:], in_=RS[:], func=AF.Sqrt, bias=epsT[:, 0:1]), reads=["RS", "epsT"], writes=["RS"])
        fw.op("vector", lambda e: e.reciprocal(out=RS[:], in_=RS[:]), reads=["RS"], writes=["RS"])
        fw.op("vector", lambda e: e.tensor_tensor(out=dst[:], in0=src[:], in1=MS[:].unsqueeze(1).to_broadcast([128, KC, N]), op=ALU.subtract),
              reads=[skey, "MS"], writes=[dkey])
        fw.op("gpsimd", lambda e: e.tensor_tensor(out=dst[:], in0=dst[:], in1=RS[:].unsqueeze(1).to_broadcast([128, KC, N]), op=ALU.mult),
              reads=[dkey, "RS"], writes=[dkey])
        for kc in range(KC):
            fw.op("vector", lambda e: e.tensor_scalar(out=dst[:, kc, :], in0=dst[:, kc, :], scalar1=LNP[:, jw, kc:kc + 1],
                                                      scalar2=LNP[:, jw + 1, kc:kc + 1], op0=ALU.mult, op1=ALU.add),
                  reads=[dkey, "LNP"], writes=[dkey])
        if dstb is not None:
            fw.op("scalar", lambda e: e.activation(out=dstb[:], in_=dst[:], func=AF.Copy), reads=[dkey], writes=[dbkey])

    for ti in range(ntile):
        t0 = ti * N
        b0, k0 = ld.next()
        fw.dma("sync", b0[:], resv[:, :, t0:t0 + N], writes=[k0])
        b1, k1 = ld.next()
        fw.dma("sync", b1[:], partv[0][:, :, t0:t0 + N], writes=[k1])
        fw.op("vector", lambda e: e.scalar_tensor_tensor(out=S[:], in0=b0[:], scalar=ALPHA, in1=b1[:], op0=ALU.mult, op1=ALU.add),
              reads=[k0, k1], writes=["S"])
        for j in range(1, len(parts)):
            bj, kj = ld.next()
            fw.dma("sync", bj[:], partv[j][:, :, t0:t0 + N], writes=[kj])
            fw.op("gpsimd" if j % 2 else "vector", lambda e: e.tensor_tensor(out=S[:], in0=S[:], in1=bj[:], op=ALU.add), reads=["S", kj], writes=["S"])
        layer_norm(S, "S", HA, "HA", 0, HAB, "HAB")
        for fc in range(FC):
            pu, pk = psu.next()
            for kc in range(KC):
                fw.op("tensor", lambda e: e.matmul(pu[:, 0:N], lhsT=W1[:, kc, fc * 128:(fc + 1) * 128], rhs=HAB[:, kc, :],
                                                   start=(kc == 0), stop=(kc == KC - 1)), reads=["HAB", "W1"], writes=[pk])
            rl, rk = RL.next()
            fw.op("scalar", lambda e: e.activation(out=rl[:], in_=pu[:, 0:N], func=AF.Relu), reads=[pk], writes=[rk])
            fw.op("gpsimd", lambda e: e.tensor_tensor(out=HID[:, fc, :], in0=rl[:], in1=rl[:], op=ALU.mult), reads=[rk], writes=["HID%d" % fc])
        for oc in range(KC):
            pd, pk = psd.next()
            for fc in range(FC):
                fw.op("tensor", lambda e: e.matmul(pd[:, 0:N], lhsT=W2[:, fc, oc * 128:(oc + 1) * 128], rhs=HID[:, fc, :],
                                                   start=(fc == 0), stop=(fc == FC - 1)), reads=["HID%d" % fc, "W2"], writes=[pk])
            fw.op("vector", lambda e: e.scalar_tensor_tensor(out=S[:, oc, :], in0=HA[:, oc, :], scalar=ALPHA, in1=pd[:, 0:N],
                                                             op0=ALU.mult, op1=ALU.add), reads=["HA", pk], writes=["S"])
        layer_norm(S, "S", OUT, "OUT", 2)
        fw.dma("sync", outv[:, :, t0:t0 + N], OUT[:], reads=["OUT"], writes=["out_dram"], semkey="st")
    fw.finish(["out_dram"])


def build_mlp(nparts):
    nc = bass.Bass("TRN2", target_bir_lowering=False)
    res = nc.dram_tensor("res", [D, TQ], F32, kind="ExternalInput").ap()
    parts = [nc.dram_tensor("part%d" % j, [D, TQ], F32, kind="ExternalInput").ap() for j in range(nparts)]
    w1 = nc.dram_tensor("w1", [D, DFF], F32, kind="ExternalInput").ap()
    w2 = nc.dram_tensor("w2", [DFF, D], F32, kind="ExternalInput").ap()
    lnp = nc.dram_tensor("lnp", [4, D], F32, kind="ExternalInput").ap()
    out = nc.dram_tensor("hout", [D, TQ], F32, kind="ExternalOutput").ap()
    fw = FW(nc)
    emit_mlp_phase(nc, fw, res, parts, w1, w2, lnp, out)
    return nc


def tri_mask(nc, fw, name, strict=False, block=None):
    m = nc.alloc_sbuf_tensor(name, [128, 128], F32)
    fw.op("gpsimd", lambda e: e.memset(m[:], 1.0), writes=[name])
    fw.op("gpsimd", lambda e: e.affine_select(out=m[:], in_=m[:], pattern=[[1, 128]], compare_op=ALU.is_gt if strict else ALU.is_ge,
                                              fill=0.0, base=0, channel_multiplier=-1), reads=[name], writes=[name])
    if block:
        fw.op("gpsimd", lambda e: e.memset(m[0:64, 64:128], 0.0), reads=[name], writes=[name])
        fw.op("gpsimd", lambda e: e.memset(m[64:128, 0:64], 0.0), reads=[name], writes=[name])
    return m


def split_quarters(t0, n):
    out = []
    t = t0
    while t < t0 + n:
        q = t // TQ
        e = min(t0 + n, (q + 1) * TQ)
        out.append((q, t - q * TQ, e - q * TQ, t - t0))
        t = e
    return out


def emit_outproj(nc, fw, Y, ykey, Wout, nkc, part, t0, bufs):
    ident, YT, PO, psT, psO = bufs
    for c in range(nkc):
        fw.op("tensor", lambda e: e.transpose(out=psT[0][:, c * 128:(c + 1) * 128], in_=Y[:, c * 128:(c + 1) * 128], identity=ident[:]),
              reads=[ykey, "ident"], writes=[psT[1]])
    fw.op("scalar", lambda e: e.activation(out=YT[:].rearrange("p c t -> p (c t)"), in_=psT[0][:, 0:nkc * 128], func=AF.Copy),
          reads=[psT[1]], writes=["YT"])
    for half in range(2):
        pb, pk = psO[half]
        for o4 in range(4):
            oc = half * 4 + o4
            for kc in range(nkc):
                fw.op("tensor", lambda e: e.matmul(pb[:, o4 * 128:(o4 + 1) * 128], lhsT=Wout[:, kc, oc * 128:(oc + 1) * 128], rhs=YT[:, kc, :],
                                                   start=(kc == 0), stop=(kc == nkc - 1)), reads=["YT", "Wout"], writes=[pk])
        fw.op("vector" if half else "scalar",
              (lambda e: e.tensor_copy(out=PO[:, half * 4:(half + 1) * 4, :].rearrange("p c t -> p (c t)"), in_=pb[:, :])) if half else
              (lambda e: e.activation(out=PO[:, half * 4:(half + 1) * 4, :].rearrange("p c t -> p (c t)"), in_=pb[:, :], func=AF.Copy)),
              reads=[pk], writes=["PO"])
    for (q, a, b, off) in split_quarters(t0, 128):
        fw.dma("sync", part[q].rearrange("(oc p) t -> p oc t", p=128)[:, :, a:b], PO[:, :, off:off + (b - a)], reads=["PO"],
               writes=["part_dram"], semkey="st")


def emit_ssd_phase(nc, fw, hT, w, convw, convb, alog, dtb, dvec, normw, wout, part, ntiles=NT):
    NCOL = 1288
    W = nc.alloc_sbuf_tensor("W", [128, KC, NCOL], BF16)
    Wout = nc.alloc_sbuf_tensor("Wout", [128, 4, D], BF16)
    wv = w.rearrange("(kc p) f -> p kc f", p=128)
    for kc in range(KC):
        fw.dma("gpsimd", W[:, kc, :], wv[:, kc, :], writes=["W"], semkey="ldw%d" % (kc % 4))
    wov = wout.rearrange("(kc p) o -> p kc o", p=128)
    for kc in range(4):
        fw.dma("gpsimd", Wout[:, kc, :], wov[:, kc, :], writes=["Wout"], semkey="ldw%d" % (kc % 4))
    CW = nc.alloc_sbuf_tensor("CW", [128, 6, 4], F32)
    CB = nc.alloc_sbuf_tensor("CB", [128, 6], F32)
    fw.dma("sync", CW[:], convw.rearrange("(c p) k -> p c k", p=128), writes=["CW"], semkey="ldc")
    fw.dma("sync", CB[:], convb.rearrange("(c p) o -> p (c o)", p=128), writes=["CB"], semkey="ldc", allow_slow_non_contiguous=True)
    AL = nc.alloc_sbuf_tensor("AL", [128, 8], F32)
    DTB = nc.alloc_sbuf_tensor("DTB", [128, 8], F32)
    DV = nc.alloc_sbuf_tensor("DV", [128, 8], F32)
    NW = nc.alloc_sbuf_tensor("NW", [128, 512], F32)
    fw.dma("sync", AL[:], alog[0:1, :].to_broadcast([128, 8]), writes=["AL"], semkey="ldc")
    fw.dma("sync", DTB[:], dtb[0:1, :].to_broadcast([128, 8]), writes=["DTB"], semkey="ldc")
    fw.dma("sync", DV[:], dvec[0:1, :].to_broadcast([128, 8]), writes=["DV"], semkey="ldc")
    fw.dma("sync", NW[:], normw[0:1, :].to_broadcast([128, 512]), writes=["NW"], semkey="ldc")
    fw.op("scalar", lambda e: e.activation(out=AL[:], in_=AL[:], func=AF.Exp), reads=["AL"], writes=["AL"])
    fw.op("vector", lambda e: e.tensor_scalar(out=AL[:], in0=AL[:], scalar1=-1.0, scalar2=None, op0=ALU.mult), reads=["AL"], writes=["AL"])
    ident = make_consts(nc, fw)["ident"]
    ones = nc.alloc_sbuf_tensor("ones", [128, 128], F32)
    fw.op("gpsimd", lambda e: e.memset(ones[:], 1.0), writes=["ones"])
    Mincl = tri_mask(nc, fw, "Mincl")
    epsT = nc.alloc_sbuf_tensor("epsr", [128, 1], F32)
    fw.op("gpsimd", lambda e: e.memset(epsT[:], 1e-6), writes=["epsr"])

    XT_IN = Rot([(nc.alloc_sbuf_tensor("xt%d" % i, [128, KC, 128], BF16), "xt%d" % i) for i in range(2)])
    PRE = nc.alloc_sbuf_tensor("PRE", [128, 6, 131], F32)
    fw.op("vector", lambda e: e.memset(PRE[:], 0.0), writes=["PRE"])
    ACC = nc.alloc_sbuf_tensor("ACC", [128, 6, 128], F32)
    XBC = nc.alloc_sbuf_tensor("XBC", [128, 6, 128], F32)
    DT8 = nc.alloc_sbuf_tensor("DT8", [128, 8], F32)
    G8 = nc.alloc_sbuf_tensor("G8", [128, 8], F32)
    GC = nc.alloc_sbuf_tensor("GC", [128, 8], F32)
    EGL = nc.alloc_sbuf_tensor("EGL", [128, 8], F32)
    WS = nc.alloc_sbuf_tensor("WS", [128, 8], F32)
    GM = nc.alloc_sbuf_tensor("GM", [128, 8, 128], F32)
    EGR = nc.alloc_sbuf_tensor("EGR", [128, 8, 128], F32)
    DM = nc.alloc_sbuf_tensor("DM", [128, 8, 128], F32)
    CBm = nc.alloc_sbuf_tensor("CBm", [128, 128], F32)
    XTM = nc.alloc_sbuf_tensor("XTM", [128, 512], F32)
    XDT = nc.alloc_sbuf_tensor("XDT", [128, 512], F32)
    XW = nc.alloc_sbuf_tensor("XW", [128, 512], F32)
    BTM = nc.alloc_sbuf_tensor("BTM", [128, 128], F32)
    SS = nc.alloc_sbuf_tensor("SS", [128, 512], F32)
    fw.op("vector", lambda e: e.memset(SS[:], 0.0), writes=["SS"])
    Y = nc.alloc_sbuf_tensor("Y", [128, 512], F32)
    T1 = nc.alloc_sbuf_tensor("T1", [128, 512], F32)
    SZ = nc.alloc_sbuf_tensor("SZ", [128, 512], F32)
    JUNK = nc.alloc_sbuf_tensor("JUNK", [128, 512], F32)
    SSQ = nc.alloc_sbuf_tensor("SSQ", [128, 1], F32)
    YT = nc.alloc_sbuf_tensor("YT", [128, 4, 128], BF16)
    PO = nc.alloc_sbuf_tensor("PO", [128, 8, 128], F32)
    b = [nc.alloc_psum_tensor("pb%d" % i, [128, 512], F32) for i in range(8)]
    fw.psum.update("b%d" % i for i in range(8))
    hv = hT.rearrange("(kc p) t -> p kc t", p=128)

    for ti in range(ntiles):
        t0 = ti * 128
        xt, xk = XT_IN.next()
        fw.dma("gpsimd", xt[:], hv[:, :, t0:t0 + 128], writes=[xk])
        if ti == 0:
            fw.op("vector", lambda e: e.memset(xt[:, :, 0:PADF], 0.0), reads=[xk], writes=[xk])
        for c in range(6):
            dst = b[0][:, c * 128:(c + 1) * 128] if c < 4 else b[1][:, (c - 4) * 128:(c - 3) * 128]
            key = "b0" if c < 4 else "b1"
            for kc in range(KC):
                fw.op("tensor", lambda e: e.matmul(dst, lhsT=W[:, kc, c * 128:(c + 1) * 128], rhs=xt[:, kc, :], start=(kc == 0), stop=(kc == KC - 1)),
                      reads=[xk, "W"], writes=[key])
        for kc in range(KC):
            fw.op("tensor", lambda e: e.matmul(b[2][:, :], lhsT=xt[:, kc, :], rhs=W[:, kc, 768:1280], start=(kc == 0), stop=(kc == KC - 1)),
                  reads=[xk, "W"], writes=["b2"])
        for kc in range(KC):
            fw.op("tensor", lambda e: e.matmul(b[7][:, 0:8], lhsT=xt[:, kc, :], rhs=W[:, kc, 1280:1288], start=(kc == 0), stop=(kc == KC - 1)),
                  reads=[xk, "W"], writes=["b7"])
        if DEBUG_STAGE < 1:
            continue
        fw.op("gpsimd", lambda e: e.tensor_copy(out=PRE[:, :, 0:3], in_=PRE[:, :, 128:131]), reads=["PRE"], writes=["PRE"])
        fw.op("scalar", lambda e: e.activation(out=PRE[:, 0:4, 3:131], in_=b[0][:, :].rearrange("p (c t) -> p c t", c=4), func=AF.Copy),
              reads=["b0"], writes=["PRE"])
        fw.op("scalar", lambda e: e.activation(out=PRE[:, 4:6, 3:131], in_=b[1][:, 0:256].rearrange("p (c t) -> p c t", c=2), func=AF.Copy),
              reads=["b1"], writes=["PRE"])
        if DEBUG_STAGE < 2:
            continue
        for c in range(6):
            fw.op("gpsimd", lambda e: e.tensor_scalar(out=ACC[:, c, :], in0=PRE[:, c, 0:128], scalar1=CW[:, c, 0:1], scalar2=None, op0=ALU.mult),
                  reads=["PRE", "CW"], writes=["ACC%d" % c])
            for k in range(1, 4):
                fw.op("vector", lambda e: e.scalar_tensor_tensor(out=ACC[:, c, :], in0=PRE[:, c, k:k + 128], scalar=CW[:, c, k:k + 1],
                                                                 in1=ACC[:, c, :], op0=ALU.mult, op1=ALU.add),
                      reads=["PRE", "CW", "ACC%d" % c], writes=["ACC%d" % c])
            fw.op("scalar", lambda e: e.activation(out=XBC[:, c, :], in_=ACC[:, c, :], func=AF.Silu, bias=CB[:, c:c + 1]),
                  reads=["ACC%d" % c, "CB"], writes=["XBC%d" % c])
        xbk = ["XBC%d" % c for c in range(6)]
        if ti == 0:
            fw.op("vector", lambda e: e.memset(XBC[:, :, 0:PADF], 0.0), reads=xbk, writes=xbk)
        if DEBUG_STAGE < 3:
            continue
        fw.op("vector", lambda e: e.tensor_tensor(out=DT8[:], in0=b[7][:, 0:8], in1=DTB[:], op=ALU.add), reads=["b7", "DTB"], writes=["DT8"])
        fw.op("scalar", lambda e: e.activation(out=DT8[:], in_=DT8[:], func=AF.Exp), reads=["DT8"], writes=["DT8"])
        fw.op("scalar", lambda e: e.activation(out=DT8[:], in_=DT8[:], func=AF.Ln, bias=1.0), reads=["DT8"], writes=["DT8"])
        fw.op("vector", lambda e: e.tensor_tensor(out=G8[:], in0=DT8[:], in1=AL[:], op=ALU.mult), reads=["DT8", "AL"], writes=["G8"])
        if DEBUG_STAGE < 4:
            continue
        fw.op("tensor", lambda e: e.matmul(b[7][:, 8:16], lhsT=Mincl[:], rhs=G8[:], start=True, stop=True), reads=["Mincl", "G8"], writes=["b7"])
        fw.op("tensor", lambda e: e.matmul(b[7][:, 16:24], lhsT=ones[:], rhs=G8[:], start=True, stop=True), reads=["ones", "G8"], writes=["b7"])
        fw.op("scalar", lambda e: e.activation(out=GC[:], in_=b[7][:, 8:16], func=AF.Copy), reads=["b7"], writes=["GC"])
        fw.op("scalar", lambda e: e.activation(out=EGL[:], in_=b[7][:, 16:24], func=AF.Exp), reads=["b7"], writes=["EGL"])
        fw.op("vector", lambda e: e.tensor_tensor(out=WS[:], in0=b[7][:, 16:24], in1=GC[:], op=ALU.subtract), reads=["b7", "GC"], writes=["WS"])
        fw.op("scalar", lambda e: e.activation(out=WS[:], in_=WS[:], func=AF.Exp), reads=["WS"], writes=["WS"])
        fw.op("vector", lambda e: e.tensor_tensor(out=WS[:], in0=WS[:], in1=DT8[:], op=ALU.mult), reads=["WS", "DT8"], writes=["WS"])
        if DEBUG_STAGE < 5:
            continue
        for h in range(8):
            fw.op("gpsimd", lambda e: e.tensor_scalar(out=GM[:, h, :], in0=Mincl[:], scalar1=G8[:, h:h + 1], scalar2=None, op0=ALU.mult),
                  reads=["Mincl", "G8"], writes=["GM"])
        if DEBUG_STAGE < 5.2:
            continue
        for hf in range(2):
            fw.op("tensor", lambda e: e.matmul(b[3 + hf][:, :], lhsT=ones[:], rhs=GM[:, hf * 4:(hf + 1) * 4, :].rearrange("p h t -> p (h t)"),
                                               start=True, stop=True), reads=["ones", "GM"], writes=["b%d" % (3 + hf)])
            fw.op("scalar", lambda e: e.activation(out=EGR[:, hf * 4:(hf + 1) * 4, :].rearrange("p h t -> p (h t)"), in_=b[3 + hf][:, :], func=AF.Exp),
                  reads=["b%d" % (3 + hf)], writes=["EGR"])
        if DEBUG_STAGE < 5.3:
            continue
        for h in range(8):
            fw.op("vector", lambda e: e.tensor_scalar(out=DM[:, h, :], in0=b[3 + h // 4][:, (h % 4) * 128:(h % 4 + 1) * 128], scalar1=GC[:, h:h + 1],
                                                      scalar2=0.0, op0=ALU.subtract, op1=ALU.min), reads=["b%d" % (3 + h // 4), "GC", "EGR"], writes=["DM"])
        if DEBUG_STAGE < 5.4:
            continue
        fw.op("scalar", lambda e: e.activation(out=DM[:].rearrange("p h t -> p (h t)"), in_=DM[:].rearrange("p h t -> p (h t)"), func=AF.Exp),
              reads=["DM"], writes=["DM"])
        if DEBUG_STAGE < 6:
            continue
        fw.op("tensor", lambda e: e.matmul(b[1][:, 256:384], lhsT=XBC[:, 4, :], rhs=XBC[:, 5, :], start=True, stop=True), reads=["XBC4", "XBC5"], writes=["b1"])
        fw.op("vector", lambda e: e.tensor_tensor(out=CBm[:], in0=b[1][:, 256:384], in1=Mincl[:], op=ALU.mult), reads=["b1", "Mincl"], writes=["CBm"])
        fw.op("gpsimd", lambda e: e.tensor_tensor(out=DM[:], in0=DM[:], in1=CBm[:].unsqueeze(1).to_broadcast([128, 8, 128]), op=ALU.mult),
              reads=["DM", "CBm"], writes=["DM"])
        fw.op("gpsimd", lambda e: e.tensor_tensor(out=EGR[:], in0=EGR[:], in1=XBC[:, 5, :].unsqueeze(1).to_broadcast([128, 8, 128]), op=ALU.mult),
              reads=["EGR", "XBC5"], writes=["EGR"])
        if DEBUG_STAGE < 7:
            continue
        for c in range(4):
            fw.op("tensor", lambda e: e.transpose(out=b[0][:, c * 128:(c + 1) * 128], in_=XBC[:, c, :], identity=ident[:]),
                  reads=["XBC%d" % c, "ident"], writes=["b0"])
        fw.op("scalar", lambda e: e.activation(out=XTM[:], in_=b[0][:, :], func=AF.Copy), reads=["b0"], writes=["XTM"])
        fw.op("tensor", lambda e: e.transpose(out=b[1][:, 384:512], in_=XBC[:, 4, :], identity=ident[:]), reads=["XBC4", "ident"], writes=["b1"])
        fw.op("scalar", lambda e: e.activation(out=BTM[:], in_=b[1][:, 384:512], func=AF.Copy), reads=["b1"], writes=["BTM"])
        X3 = XTM[:].rearrange("p (h d) -> p h d", h=8)
        fw.op("vector", lambda e: e.tensor_tensor(out=XDT[:].rearrange("p (h d) -> p h d", h=8), in0=X3, in1=DT8[:].unsqueeze(2).to_broadcast([128, 8, 64]),
                                                  op=ALU.mult), reads=["XTM", "DT8"], writes=["XDT"])
        fw.op("gpsimd", lambda e: e.tensor_tensor(out=XW[:].rearrange("p (h d) -> p h d", h=8), in0=X3, in1=WS[:].unsqueeze(2).to_broadcast([128, 8, 64]),
                                                  op=ALU.mult), reads=["XTM", "WS"], writes=["XW"])
        if DEBUG_STAGE < 8:
            continue
        for h in range(8):
            fw.op("tensor", lambda e: e.matmul(b[5][:, h * 64:(h + 1) * 64], lhsT=DM[:, h, :], rhs=XDT[:, h * 64:(h + 1) * 64], start=True, stop=False),
                  reads=["DM", "XDT"], writes=["b5"])
            fw.op("tensor", lambda e: e.matmul(b[5][:, h * 64:(h + 1) * 64], lhsT=EGR[:, h, :], rhs=SS[:, h * 64:(h + 1) * 64], start=False, stop=True),
                  reads=["EGR", "SS"], writes=["b5"])
        if DEBUG_STAGE < 9:
            continue
        fw.op("tensor", lambda e: e.matmul(b[6][:, :], lhsT=BTM[:], rhs=XW[:], start=True, stop=True), reads=["BTM", "XW"], writes=["b6"])
        fw.op("vector", lambda e: e.tensor_tensor(out=SS[:].rearrange("p (h d) -> p h d", h=8), in0=SS[:].rearrange("p (h d) -> p h d", h=8),
                                                  in1=EGL[:].unsqueeze(2).to_broadcast([128, 8, 64]), op=ALU.mult), reads=["SS", "EGL"], writes=["SS"])
        fw.op("vector", lambda e: e.tensor_tensor(out=SS[:], in0=SS[:], in1=b[6][:, :], op=ALU.add), reads=["SS", "b6"], writes=["SS"])
        if DEBUG_STAGE < 10:
            continue
        fw.op("gpsimd", lambda e: e.tensor_tensor(out=T1[:].rearrange("p (h d) -> p h d", h=8), in0=X3, in1=DV[:].unsqueeze(2).to_broadcast([128, 8, 64]),
                                                  op=ALU.mult), reads=["XTM", "DV"], writes=["T1"])
        fw.op("vector", lambda e: e.tensor_tensor(out=Y[:], in0=b[5][:, :], in1=T1[:], op=ALU.add), reads=["b5", "T1"], writes=["Y"])
        fw.op("scalar", lambda e: e.activation(out=SZ[:], in_=b[2][:, :], func=AF.Silu), reads=["b2"], writes=["SZ"])
        fw.op("gpsimd", lambda e: e.tensor_tensor(out=Y[:], in0=Y[:], in1=SZ[:], op=ALU.mult), reads=["Y", "SZ"], writes=["Y"])
        fw.op("scalar", lambda e: e.activation(out=JUNK[:], in_=Y[:], func=AF.Square, accum_out=SSQ[:, 0:1]), reads=["Y"], writes=["JUNK", "SSQ"])
        fw.op("scalar", lambda e: e.activation(out=SSQ[:], in_=SSQ[:], func=AF.Sqrt, scale=1.0 / 512, bias=epsT[:, 0:1]), reads=["SSQ", "epsr"], writes=["SSQ"])
        fw.op("vector", lambda e: e.reciprocal(out=SSQ[:], in_=SSQ[:]), reads=["SSQ"], writes=["SSQ"])
        fw.op("vector", lambda e: e.scalar_tensor_tensor(out=Y[:], in0=Y[:], scalar=SSQ[:, 0:1], in1=NW[:], op0=ALU.mult, op1=ALU.mult),
              reads=["Y", "SSQ", "NW"], writes=["Y"])
        emit_outproj(nc, fw, Y, "Y", Wout, 4, part, t0, (ident, YT, PO, (b[0], "b0"), [(b[3], "b3"), (b[4], "b4")]))
    fw.finish(["part_dram"])


def build_ssd(ntiles=NT):
    nc = bass.Bass("TRN2", target_bir_lowering=False)
    hT = nc.dram_tensor("hT", [D, R], F32, kind="ExternalInput").ap()
    w = nc.dram_tensor("w", [D, 1288], F32, kind="ExternalInput").ap()
    convw = nc.dram_tensor("convw", [768, 4], F32, kind="ExternalInput").ap()
    convb = nc.dram_tensor("convb", [768, 1], F32, kind="ExternalInput").ap()
    alog = nc.dram_tensor("alog", [1, 8], F32, kind="ExternalInput").ap()
    dtb = nc.dram_tensor("dtb", [1, 8], F32, kind="ExternalInput").ap()
    dvec = nc.dram_tensor("dvec", [1, 8], F32, kind="ExternalInput").ap()
    normw = nc.dram_tensor("normw", [1, 512], F32, kind="ExternalInput").ap()
    wout = nc.dram_tensor("wout", [512, D], F32, kind="ExternalInput").ap()
    part = nc.dram_tensor("part", [4, D, TQ], F32, kind="ExternalOutput").ap()
    fw = FW(nc)
    emit_ssd_phase(nc, fw, hT, w, convw, convb, alog, dtb, dvec, normw, wout, part, ntiles)
    return nc


def ssd_inputs(inp, hT_b, g):
    cw = inp["c_w_in"][0]
    xs0, b0, c0, dt0 = 2048, 2048 + 2048, 2048 + 2048 + 512, 2048 + 3072
    cols = np.concatenate([np.arange(xs0 + 512 * g, xs0 + 512 * g + 512), np.arange(b0 + 128 * g, b0 + 128 * g + 128),
                           np.arange(c0 + 128 * g, c0 + 128 * g + 128), np.arange(512 * g, 512 * g + 512), np.arange(dt0 + 8 * g, dt0 + 8 * g + 8)])
    ccols = cols[:768] - 2048
    return {
        "hT": hT_b,
        "w": np.ascontiguousarray(cw[:, cols]),
        "convw": np.ascontiguousarray(inp["c_conv_w"][0][:, ccols].T),
        "convb": np.ascontiguousarray(inp["c_conv_b"][0][ccols].reshape(768, 1)),
        "alog": np.ascontiguousarray(inp["c_A_log"][0][8 * g:8 * g + 8].reshape(1, 8)),
        "dtb": np.ascontiguousarray(inp["c_dt_bias"][0][8 * g:8 * g + 8].reshape(1, 8)),
        "dvec": np.ascontiguousarray(inp["c_D"][0][8 * g:8 * g + 8].reshape(1, 8)),
        "normw": np.ascontiguousarray(inp["c_norm_w"][0][512 * g:512 * g + 512].reshape(1, 512)),
        "wout": np.ascontiguousarray(inp["c_w_out"][0][512 * g:512 * g + 512, :]),
    }


NCOL_AB = 1024 + 512 + 259
DK = 128
QSCALE = float(DK ** -0.5)


def emit_ab_phase(nc, fw, xT, w, cs, convw, rc, gnr, gng, alog, dtb, wout, part, ntiles=NT):
    W = nc.alloc_sbuf_tensor("W", [128, KC, NCOL_AB], BF16)
    Wout = nc.alloc_sbuf_tensor("Wout", [128, 4, D], BF16)
    wv = w.rearrange("(kc p) f -> p kc f", p=128)
    for kc in range(KC):
        fw.dma("gpsimd", W[:, kc, :], wv[:, kc, :], writes=["W"], semkey="ldw%d" % (kc % 4))
    wov = wout.rearrange("(kc p) o -> p kc o", p=128)
    for kc in range(4):
        fw.dma("gpsimd", Wout[:, kc, :], wov[:, kc, :], writes=["Wout"], semkey="ldw%d" % (kc % 4))
    CW = nc.alloc_sbuf_tensor("CW", [128, 4, 4], F32)
    fw.dma("sync", CW[:], convw.rearrange("(c p) k -> p c k", p=128), writes=["CW"], semkey="ldc")
    RC = nc.alloc_sbuf_tensor("RC", [128, 258], F32)
    fw.dma("sync", RC[:], rc[:, :], writes=["RC"], semkey="ldc")
    maskR, qdec, kdecv, g128 = RC[:, 0:128], RC[:, 128:256], RC[:, 256:257], RC[:, 257:258]
    GNR = nc.alloc_sbuf_tensor("GNR", [128, 256], F32)
    GNG = nc.alloc_sbuf_tensor("GNG", [128, 256], F32)
    NEGA = nc.alloc_sbuf_tensor("NEGA", [128, 1], F32)
    DTB = nc.alloc_sbuf_tensor("DTB", [128, 1], F32)
    fw.dma("sync", GNR[:], gnr[0:1, :].to_broadcast([128, 256]), writes=["GNR"], semkey="ldc")
    fw.dma("sync", GNG[:], gng[0:1, :].to_broadcast([128, 256]), writes=["GNG"], semkey="ldc")
    fw.dma("sync", NEGA[:], alog[0:1, :].to_broadcast([128, 1]), writes=["NEGA"], semkey="ldc")
    fw.dma("sync", DTB[:], dtb[0:1, :].to_broadcast([128, 1]), writes=["DTB"], semkey="ldc")
    fw.op("scalar", lambda e: e.activation(out=NEGA[:], in_=NEGA[:], func=AF.Exp), reads=["NEGA"], writes=["NEGA"])
    fw.op("vector", lambda e: e.tensor_scalar(out=NEGA[:], in0=NEGA[:], scalar1=-1.0, scalar2=None, op0=ALU.mult), reads=["NEGA"], writes=["NEGA"])
    ident = make_consts(nc, fw)["ident"]
    ones = nc.alloc_sbuf_tensor("ones", [128, 128], F32)
    fw.op("gpsimd", lambda e: e.memset(ones[:], 1.0), writes=["ones"])
    MinclB = tri_mask(nc, fw, "MinclB", strict=False, block=True)
    MSU = tri_mask(nc, fw, "MSU", strict=True, block=True)
    Msame = nc.alloc_sbuf_tensor("Msame", [128, 128], F32)
    fw.op("gpsimd", lambda e: e.memset(Msame[:], 0.0), writes=["Msame"])
    fw.op("gpsimd", lambda e: e.memset(Msame[0:64, 0:64], 1.0), reads=["Msame"], writes=["Msame"])
    fw.op("gpsimd", lambda e: e.memset(Msame[64:128, 64:128], 1.0), reads=["Msame"], writes=["Msame"])
    CHIND = nc.alloc_sbuf_tensor("CHIND", [128, 2], F32)
    fw.op("gpsimd", lambda e: e.memset(CHIND[:], 0.0), writes=["CHIND"])
    fw.op("gpsimd", lambda e: e.memset(CHIND[0:64, 0:1], 1.0), reads=["CHIND"], writes=["CHIND"])
    fw.op("gpsimd", lambda e: e.memset(CHIND[64:128, 1:2], 1.0), reads=["CHIND"], writes=["CHIND"])
    eps5 = nc.alloc_sbuf_tensor("eps5", [128, 1], F32)
    eps6 = nc.alloc_sbuf_tensor("eps6", [128, 1], F32)
    fw.op("gpsimd", lambda e: e.memset(eps5[:], 1e-5), writes=["eps5"])
    fw.op("gpsimd", lambda e: e.memset(eps6[:], 1e-6), writes=["eps6"])

    def sb(name, shape, dt=F32):
        return nc.alloc_sbuf_tensor(name, shape, dt)

    XT_IN = Rot([(sb("xt%d" % i, [128, KC, 128], BF16), "xt%d" % i) for i in range(2)])
    CS = Rot([(sb("cs%d" % i, [128, 2, 128]), "cs%d" % i) for i in range(2)])
    T1, T2, QR, KR, ATR, QD, KDEC = [sb(n, [128, 128]) for n in ["T1", "T2", "QR", "KR", "ATR", "QD", "KDEC"]]
    VR = sb("VR", [128, 256])
    SR = sb("SR", [128, 256])
    SGS = sb("SGS", [128, 256])
    fw.op("vector", lambda e: e.memset(SR[:], 0.0), writes=["SR"])
    fw.op("vector", lambda e: e.memset(SGS[:], 0.0), writes=["SGS"])
    BST = sb("BST", [128, 6])
    MV = sb("MV", [128, 2])
    RSTD = sb("RSTD", [128, 1])
    Y = sb("Y", [128, 512])
    SIL = sb("SIL", [128, 256])
    PRE = sb("PRE", [128, 4, 131])
    fw.op("vector", lambda e: e.memset(PRE[:], 0.0), writes=["PRE"])
    ACC = sb("ACC", [128, 4, 128])
    ACT4 = sb("ACT4", [128, 4, 128])
    SQ2 = sb("SQ2", [128, 256])
    RN = sb("RN", [128, 256])
    GKQ = sb("GKQ", [128, 2, 128])
    BETA = sb("BETA", [128, 1])
    E2 = sb("E2", [128, 2])
    G2 = sb("G2", [128, 2])
    GSEL = sb("GSEL", [128, 2])
    GC = sb("GC", [128, 1])
    NEGEG = sb("NEGEG", [128, 1])
    KSC = sb("KSC", [128, 1])
    EGL2 = sb("EGL2", [128, 2])
    GON, EGRW, QT, DMt, M1, M2, APT, ATG, APM, Pm, Qm, X2, X2T, KDG = [sb(n, [128, 128]) for n in
        ["GON", "EGRW", "QT", "DMt", "M1", "M2", "APT", "ATG", "APM", "Pm", "Qm", "X2", "X2T", "KDG"]]
    VG = sb("VG", [128, 256])
    RR = sb("RR", [128, 256])
    VN = sb("VN", [128, 256])
    JUNK = sb("JUNK", [128, 256])
    SSQ = sb("SSQ", [128, 1])
    YT = sb("YT", [128, 4, 128], BF16)
    PO = sb("PO", [128, 8, 128])
    b = [nc.alloc_psum_tensor("pb%d" % i, [128, 512], F32) for i in range(8)]
    fw.psum.update("b%d" % i for i in range(8))
    hv = xT.rearrange("(kc p) t -> p kc t", p=128)
    csv = cs.rearrange("c p t -> p c t")

    V = lambda fn, r, w_: fw.op("vector", fn, reads=r, writes=w_)
    A = lambda fn, r, w_: fw.op("scalar", fn, reads=r, writes=w_)
    G = lambda fn, r, w_: fw.op("gpsimd", fn, reads=r, writes=w_)
    T = lambda fn, r, w_: fw.op("tensor", fn, reads=r, writes=w_)

    for ti in range(ntiles):
        t0 = ti * 128
        xt, xk = XT_IN.next()
        fw.dma("gpsimd", xt[:], hv[:, :, t0:t0 + 128], writes=[xk])
        cst, ck = CS.next()
        fw.dma("sync", cst[:], csv[:, :, t0:t0 + 128], writes=[ck])
        for c in range(8):
            dst = b[c // 4][:, (c % 4) * 128:(c % 4 + 1) * 128]
            for kc in range(KC):
                T(lambda e: e.matmul(dst, lhsT=W[:, kc, c * 128:(c + 1) * 128], rhs=xt[:, kc, :], start=(kc == 0), stop=(kc == KC - 1)),
                  [xk, "W"], ["b%d" % (c // 4)])
        for kc in range(KC):
            T(lambda e: e.matmul(b[2][:, :], lhsT=xt[:, kc, :], rhs=W[:, kc, 1024:1536], start=(kc == 0), stop=(kc == KC - 1)), [xk, "W"], ["b2"])
        for kc in range(KC):
            T(lambda e: e.matmul(b[7][:, 0:259], lhsT=xt[:, kc, :], rhs=W[:, kc, 1536:1795], start=(kc == 0), stop=(kc == KC - 1)), [xk, "W"], ["b7"])
        if DEBUG_STAGE < 1:
            continue
        V(lambda e: e.tensor_tensor(out=T1[:], in0=b[0][:, 0:128], in1=cst[:, 0, :], op=ALU.mult), ["b0", ck], ["T1"])
        V(lambda e: e.tensor_tensor(out=T2[:], in0=b[0][:, 128:256], in1=cst[:, 1, :], op=ALU.mult), ["b0", ck], ["T2"])
        G(lambda e: e.tensor_tensor(out=QR[:], in0=T1[:], in1=T2[:], op=ALU.add), ["T1", "T2"], ["QR"])
        V(lambda e: e.tensor_tensor(out=T1[:], in0=b[0][:, 256:384], in1=cst[:, 0, :], op=ALU.mult), ["b0", ck], ["T1"])
        V(lambda e: e.tensor_tensor(out=T2[:], in0=b[0][:, 384:512], in1=cst[:, 1, :], op=ALU.mult), ["b0", ck], ["T2"])
        G(lambda e: e.tensor_tensor(out=KR[:], in0=T1[:], in1=T2[:], op=ALU.add), ["T1", "T2"], ["KR"])
        T(lambda e: e.matmul(b[3][:, 0:128], lhsT=KR[:], rhs=QR[:], start=True, stop=True), ["KR", "QR"], ["b3"])
        V(lambda e: e.tensor_tensor(out=ATR[:], in0=b[3][:, 0:128], in1=maskR, op=ALU.mult), ["b3", "RC"], ["ATR"])
        A(lambda e: e.activation(out=VR[:], in_=b[2][:, 0:256], func=AF.Copy), ["b2"], ["VR"])
        G(lambda e: e.tensor_tensor(out=QD[:], in0=QR[:], in1=qdec, op=ALU.mult), ["QR", "RC"], ["QD"])
        T(lambda e: e.matmul(b[4][:, 0:256], lhsT=ATR[:], rhs=VR[:], start=True, stop=False), ["ATR", "VR"], ["b4"])
        T(lambda e: e.matmul(b[4][:, 0:256], lhsT=QD[:], rhs=SR[:], start=False, stop=True), ["QD", "SR"], ["b4"])
        T(lambda e: e.transpose(out=b[3][:, 128:256], in_=KR[:], identity=ident[:]), ["KR", "ident"], ["b3"])
        A(lambda e: e.activation(out=KDEC[:], in_=b[3][:, 128:256], func=AF.Copy, scale=kdecv), ["b3", "RC"], ["KDEC"])
        T(lambda e: e.matmul(b[5][:, 0:256], lhsT=KDEC[:], rhs=VR[:], start=True, stop=True), ["KDEC", "VR"], ["b5"])
        V(lambda e: e.scalar_tensor_tensor(out=SR[:], in0=SR[:], scalar=g128, in1=b[5][:, 0:256], op0=ALU.mult, op1=ALU.add), ["SR", "RC", "b5"], ["SR"])
        V(lambda e: e.bn_stats(out=BST[:], in_=b[4][:, 0:256]), ["b4"], ["BST"])
        V(lambda e: e.bn_aggr(out=MV[:], in_=BST[:]), ["BST"], ["MV"])
        A(lambda e: e.activation(out=RSTD[:], in_=MV[:, 1:2], func=AF.Sqrt, bias=eps5[:, 0:1]), ["MV", "eps5"], ["RSTD"])
        V(lambda e: e.reciprocal(out=RSTD[:], in_=RSTD[:]), ["RSTD"], ["RSTD"])
        V(lambda e: e.tensor_scalar(out=Y[:, 0:256], in0=b[4][:, 0:256], scalar1=MV[:, 0:1], scalar2=RSTD[:, 0:1], op0=ALU.subtract, op1=ALU.mult),
          ["b4", "MV", "RSTD"], ["Y0"])
        G(lambda e: e.tensor_tensor(out=Y[:, 0:256], in0=Y[:, 0:256], in1=GNR[:], op=ALU.mult), ["Y0", "GNR"], ["Y0"])
        A(lambda e: e.activation(out=SIL[:], in_=b[2][:, 256:512], func=AF.Silu), ["b2"], ["SIL"])
        G(lambda e: e.tensor_tensor(out=Y[:, 0:256], in0=Y[:, 0:256], in1=SIL[:], op=ALU.mult), ["Y0", "SIL"], ["Y0"])
        if DEBUG_STAGE < 2:
            continue
        G(lambda e: e.tensor_copy(out=PRE[:, :, 0:3], in_=PRE[:, :, 128:131]), ["PRE"], ["PRE"])
        A(lambda e: e.activation(out=PRE[:, :, 3:131], in_=b[1][:, :].rearrange("p (c t) -> p c t", c=4), func=AF.Copy), ["b1"], ["PRE"])
        for c in range(4):
            G(lambda e: e.tensor_scalar(out=ACC[:, c, :], in0=PRE[:, c, 0:128], scalar1=CW[:, c, 0:1], scalar2=None, op0=ALU.mult),
              ["PRE", "CW"], ["ACC%d" % c])
            for k in range(1, 4):
                V(lambda e: e.scalar_tensor_tensor(out=ACC[:, c, :], in0=PRE[:, c, k:k + 128], scalar=CW[:, c, k:k + 1], in1=ACC[:, c, :],
                                                   op0=ALU.mult, op1=ALU.add), ["PRE", "CW", "ACC%d" % c], ["ACC%d" % c])
        acck = ["ACC%d" % c for c in range(4)]
        A(lambda e: e.activation(out=ACT4[:].rearrange("p c t -> p (c t)"), in_=ACC[:].rearrange("p c t -> p (c t)"), func=AF.Silu), acck, ["ACT4"])
        A(lambda e: e.activation(out=SQ2[:], in_=ACT4[:, 0:2, :].rearrange("p c t -> p (c t)"), func=AF.Square), ["ACT4"], ["SQ2"])
        T(lambda e: e.matmul(b[6][:, 0:256], lhsT=ones[:], rhs=SQ2[:], start=True, stop=True), ["ones", "SQ2"], ["b6"])
        A(lambda e: e.activation(out=RN[:], in_=b[6][:, 0:256], func=AF.Sqrt, bias=eps6[:, 0:1]), ["b6", "eps6"], ["RN"])
        V(lambda e: e.reciprocal(out=RN[:], in_=RN[:]), ["RN"], ["RN"])
        V(lambda e: e.scalar_tensor_tensor(out=GKQ[:, 1, :], in0=ACT4[:, 0, :], scalar=QSCALE, in1=RN[:, 0:128], op0=ALU.mult, op1=ALU.mult),
          ["ACT4", "RN"], ["GQ"])
        G(lambda e: e.tensor_tensor(out=GKQ[:, 0, :], in0=ACT4[:, 1, :], in1=RN[:, 128:256], op=ALU.mult), ["ACT4", "RN"], ["GK"])
        if DEBUG_STAGE < 3:
            continue
        A(lambda e: e.activation(out=BETA[:], in_=b[7][:, 256:257], func=AF.Sigmoid), ["b7"], ["BETA"])
        A(lambda e: e.activation(out=E2[:], in_=b[7][:, 257:259], func=AF.Exp, bias=DTB[:, 0:1]), ["b7", "DTB"], ["E2"])
        A(lambda e: e.activation(out=E2[:], in_=E2[:], func=AF.Ln, bias=1.0), ["E2"], ["E2"])
        V(lambda e: e.tensor_scalar(out=G2[:], in0=E2[:], scalar1=NEGA[:, 0:1], scalar2=None, op0=ALU.mult), ["E2", "NEGA"], ["G2"])
        V(lambda e: e.tensor_scalar(out=GSEL[:], in0=CHIND[:], scalar1=G2[:, 0:1], scalar2=None, op0=ALU.mult), ["CHIND", "G2"], ["GSEL"])
        T(lambda e: e.matmul(b[7][:, 260:262], lhsT=MinclB[:], rhs=G2[:], start=True, stop=True), ["MinclB", "G2"], ["b7"])
        T(lambda e: e.matmul(b[7][:, 262:264], lhsT=Msame[:], rhs=G2[:], start=True, stop=True), ["Msame", "G2"], ["b7"])
        T(lambda e: e.matmul(b[7][:, 264:266], lhsT=ones[:], rhs=GSEL[:], start=True, stop=True), ["ones", "GSEL"], ["b7"])
        A(lambda e: e.activation(out=GC[:], in_=b[7][:, 260:261], func=AF.Copy), ["b7"], ["GC"])
        A(lambda e: e.activation(out=NEGEG[:], in_=b[7][:, 260:261], func=AF.Exp), ["b7"], ["NEGEG"])
        V(lambda e: e.tensor_scalar(out=NEGEG[:], in0=NEGEG[:], scalar1=-1.0, scalar2=None, op0=ALU.mult), ["NEGEG"], ["NEGEG"])
        V(lambda e: e.tensor_tensor(out=KSC[:], in0=b[7][:, 262:263], in1=GC[:], op=ALU.subtract), ["b7", "GC"], ["KSC"])
        A(lambda e: e.activation(out=KSC[:], in_=KSC[:], func=AF.Exp), ["KSC"], ["KSC"])
        A(lambda e: e.activation(out=EGL2[:], in_=b[7][:, 264:266], func=AF.Exp), ["b7"], ["EGL2"])
        A(lambda e: e.activation(out=GON[:], in_=ones[:], func=AF.Copy, scale=G2[:, 0:1]), ["ones", "G2"], ["GON"])
        T(lambda e: e.matmul(b[6][:, 256:384], lhsT=GON[:], rhs=MinclB[:], start=True, stop=True), ["GON", "MinclB"], ["b6"])
        A(lambda e: e.activation(out=EGRW[:], in_=b[6][:, 256:384], func=AF.Exp), ["b6"], ["EGRW"])
        G(lambda e: e.tensor_tensor(out=QT[:], in0=GKQ[:, 1, :], in1=EGRW[:], op=ALU.mult), ["GQ", "EGRW"], ["QT"])
        V(lambda e: e.tensor_scalar(out=DMt[:], in0=b[6][:, 256:384], scalar1=GC[:, 0:1], scalar2=0.0, op0=ALU.subtract, op1=ALU.min), ["b6", "GC"], ["DMt"])
        A(lambda e: e.activation(out=DMt[:], in_=DMt[:], func=AF.Exp), ["DMt"], ["DMt"])
        G(lambda e: e.tensor_tensor(out=M1[:], in0=DMt[:], in1=MSU[:], op=ALU.mult), ["DMt", "MSU"], ["M1"])
        G(lambda e: e.tensor_tensor(out=M2[:], in0=DMt[:], in1=MinclB[:], op=ALU.mult), ["DMt", "MinclB"], ["M2"])
        T(lambda e: e.matmul(b[3][:, 256:512], lhsT=GKQ[:, 0, :], rhs=GKQ[:].rearrange("p c t -> p (c t)"), start=True, stop=True), ["GK", "GQ"], ["b3"])
        V(lambda e: e.scalar_tensor_tensor(out=APT[:], in0=b[3][:, 256:384], scalar=BETA[:, 0:1], in1=M1[:], op0=ALU.mult, op1=ALU.mult),
          ["b3", "BETA", "M1"], ["APT"])
        V(lambda e: e.tensor_tensor(out=ATG[:], in0=b[3][:, 384:512], in1=M2[:], op=ALU.mult), ["b3", "M2"], ["ATG"])
        if DEBUG_STAGE < 4:
            continue
        T(lambda e: e.transpose(out=b[5][:, 256:384], in_=APT[:], identity=ident[:]), ["APT", "ident"], ["b5"])
        A(lambda e: e.activation(out=APM[:], in_=b[5][:, 256:384], func=AF.Copy), ["b5"], ["APM"])
        G(lambda e: e.tensor_tensor(out=Pm[:], in0=ident[:], in1=APM[:], op=ALU.subtract), ["ident", "APM"], ["Pm"])
        G(lambda e: e.tensor_tensor(out=Qm[:], in0=ident[:], in1=APT[:], op=ALU.subtract), ["ident", "APT"], ["Qm"])
        X, Xk, XT_, XTk = APM, "APM", APT, "APT"
        for lvl in range(5):
            last = lvl == 4
            T(lambda e: e.matmul(b[0][:, 0:128], lhsT=X[:], rhs=XT_[:], start=True, stop=True), [Xk, XTk], ["b0"])
            if not last:
                T(lambda e: e.matmul(b[0][:, 128:256], lhsT=XT_[:], rhs=X[:], start=True, stop=True), [Xk, XTk], ["b0"])
            A(lambda e: e.activation(out=X2T[:], in_=b[0][:, 0:128], func=AF.Copy), ["b0"], ["X2T"])
            if not last:
                A(lambda e: e.activation(out=X2[:], in_=b[0][:, 128:256], func=AF.Copy), ["b0"], ["X2"])
            T(lambda e: e.matmul(b[1][:, 0:128], lhsT=Pm[:], rhs=X2T[:], start=True, stop=True), ["Pm", "X2T"], ["b1"])
            if not last:
                T(lambda e: e.matmul(b[1][:, 128:256], lhsT=Qm[:], rhs=X2[:], start=True, stop=True), ["Qm", "X2"], ["b1"])
            V(lambda e: e.tensor_tensor(out=Qm[:], in0=Qm[:], in1=b[1][:, 0:128], op=ALU.add), ["Qm", "b1"], ["Qm"])
            if not last:
                V(lambda e: e.tensor_tensor(out=Pm[:], in0=Pm[:], in1=b[1][:, 128:256], op=ALU.add), ["Pm", "b1"], ["Pm"])
            X, Xk, XT_, XTk = X2, "X2", X2T, "X2T"
        T(lambda e: e.transpose(out=b[2][:, 0:128], in_=ACT4[:, 2, :], identity=ident[:]), ["ACT4", "ident"], ["b2"])
        T(lambda e: e.transpose(out=b[2][:, 128:256], in_=ACT4[:, 3, :], identity=ident[:]), ["ACT4", "ident"], ["b2"])
        T(lambda e: e.transpose(out=b[2][:, 256:384], in_=GKQ[:, 0, :], identity=ident[:]), ["GK", "ident"], ["b2"])
        A(lambda e: e.activation(out=VG[:], in_=b[2][:, 0:256], func=AF.Copy), ["b2"], ["VG"])
        A(lambda e: e.activation(out=KDG[:], in_=b[2][:, 256:384], func=AF.Copy, scale=KSC[:, 0:1]), ["b2", "KSC"], ["KDG"])
        if DEBUG_STAGE < 5:
            continue
        for c in range(2):
            hs = slice(64 * c, 64 * c + 64)
            T(lambda e: e.matmul(b[3][hs, 0:256], lhsT=GKQ[:, 0, hs], rhs=SGS[:], start=True, stop=True), ["GK", "SGS"], ["b3"])
            V(lambda e: e.scalar_tensor_tensor(out=RR[hs, :], in0=b[3][hs, 0:256], scalar=NEGEG[hs, 0:1], in1=VG[hs, :], op0=ALU.mult, op1=ALU.add),
              ["b3", "NEGEG", "VG"], ["RR"])
            T(lambda e: e.matmul(b[4][hs, 0:256], lhsT=Qm[hs, hs], rhs=RR[hs, :], start=True, stop=True), ["Qm", "RR"], ["b4"])
            V(lambda e: e.tensor_scalar(out=VN[hs, :], in0=b[4][hs, 0:256], scalar1=BETA[hs, 0:1], scalar2=None, op0=ALU.mult), ["b4", "BETA"], ["VN"])
            T(lambda e: e.matmul(b[5][hs, 0:256], lhsT=QT[:, hs], rhs=SGS[:], start=True, stop=False), ["QT", "SGS"], ["b5"])
            T(lambda e: e.matmul(b[5][hs, 0:256], lhsT=ATG[hs, hs], rhs=VN[hs, :], start=False, stop=True), ["ATG", "VN"], ["b5"])
            T(lambda e: e.matmul(b[6][:, 0:256], lhsT=KDG[hs, :], rhs=VN[hs, :], start=True, stop=True), ["KDG", "VN"], ["b6"])
            V(lambda e: e.scalar_tensor_tensor(out=SGS[:], in0=SGS[:], scalar=EGL2[:, c:c + 1], in1=b[6][:, 0:256], op0=ALU.mult, op1=ALU.add),
              ["SGS", "EGL2", "b6"], ["SGS"])
        A(lambda e: e.activation(out=JUNK[:], in_=b[5][:, 0:256], func=AF.Square, accum_out=SSQ[:, 0:1]), ["b5"], ["JUNK", "SSQ"])
        A(lambda e: e.activation(out=SSQ[:], in_=SSQ[:], func=AF.Sqrt, scale=1.0 / 256, bias=eps6[:, 0:1]), ["SSQ", "eps6"], ["SSQ"])
        V(lambda e: e.reciprocal(out=SSQ[:], in_=SSQ[:]), ["SSQ"], ["SSQ"])
        V(lambda e: e.scalar_tensor_tensor(out=Y[:, 256:512], in0=b[5][:, 0:256], scalar=SSQ[:, 0:1], in1=GNG[:], op0=ALU.mult, op1=ALU.mult),
          ["b5", "SSQ", "GNG"], ["Y1"])
        A(lambda e: e.activation(out=SIL[:], in_=b[7][:, 0:256], func=AF.Silu), ["b7"], ["SIL"])
        G(lambda e: e.tensor_tensor(out=Y[:, 256:512], in0=Y[:, 256:512], in1=SIL[:], op=ALU.mult), ["Y1", "SIL"], ["Y1"])
        fw.op("gpsimd", lambda e: e.tensor_copy(out=Y[:, 0:1], in_=Y[:, 0:1]), reads=["Y0", "Y1"], writes=["Y"])
        emit_outproj(nc, fw, Y, "Y", Wout, 4, part, t0, (ident, YT, PO, (b[0], "b0"), [(b[1], "b1"), (b[2], "b2")]))
    fw.finish(["part_dram"])


def build_ab(ntiles=NT):
    nc = bass.Bass("TRN2", target_bir_lowering=False)
    xT = nc.dram_tensor("xT", [D, R], F32, kind="ExternalInput").ap()
    w = nc.dram_tensor("w", [D, NCOL_AB], F32, kind="ExternalInput").ap()
    cs = nc.dram_tensor("cs", [2, 128, R], F32, kind="ExternalInput").ap()
    convw = nc.dram_tensor("convw", [512, 4], F32, kind="ExternalInput").ap()
    rc = nc.dram_tensor("rc", [128, 258], F32, kind="ExternalInput").ap()
    gnr = nc.dram_tensor("gnr", [1, 256], F32, kind="ExternalInput").ap()
    gng = nc.dram_tensor("gng", [1, 256], F32, kind="ExternalInput").ap()
    alog = nc.dram_tensor("alog", [1, 1], F32, kind="ExternalInput").ap()
    dtb = nc.dram_tensor("dtb", [1, 1], F32, kind="ExternalInput").ap()
    wout = nc.dram_tensor("wout", [512, D], F32, kind="ExternalInput").ap()
    part = nc.dram_tensor("part", [4, D, TQ], F32, kind="ExternalOutput").ap()
    fw = FW(nc)
    emit_ab_phase(nc, fw, xT, w, cs, convw, rc, gnr, gng, alog, dtb, wout, part, ntiles)
    return nc


def rope_tables():
    pos = (np.arange(R, dtype=np.float32) - np.float32(PADF)).astype(np.float32)
    inv_freq = (1.0 / (np.float32(10000.0) ** np.linspace(0.0, 1.0, 64, dtype=np.float32))).astype(np.float32)
    ang = (pos[None, :] * inv_freq[:, None]).astype(np.float32)
    c = np.cos(ang).astype(np.float32)
    s = np.sin(ang).astype(np.float32)
    return np.ascontiguousarray(np.stack([np.concatenate([c, c], 0), np.concatenate([-s, s], 0)], 0))


def ret_consts(j):
    lg = np.log1p(-np.exp2(-5.0 - j))
    idx = np.arange(128)
    diff = idx[None, :] - idx[:, None]
    maskR = np.where(diff >= 0, np.exp(lg * np.maximum(diff, 0)), 0.0) * QSCALE
    qdec = np.broadcast_to((np.exp(lg * (idx + 1)) * QSCALE)[None, :], (128, 128))
    kdecv = np.exp(lg * (127 - idx))[:, None]
    g128 = np.full((128, 1), np.exp(lg * 128))
    return np.ascontiguousarray(np.concatenate([maskR, qdec, kdecv, g128], 1).astype(np.float32))


def ab_inputs(inp, xT_b, j, cs_tab):
    w = inp["ab_w_in"][0]
    o_rq, o_rk, o_rv, o_rg, o_gq, o_gk, o_gv, o_gb, o_ga, o_gg = 0, 512, 1024, 2048, 3072, 3584, 4096, 5120, 5124, 5128
    a = np.arange
    sw = np.concatenate([a(64, 128), a(0, 64)])
    cols = np.concatenate([
        o_rq + 128 * j + a(128), o_rq + 128 * j + sw, o_rk + 128 * j + a(128), o_rk + 128 * j + sw,
        o_gq + 128 * j + a(128), o_gk + 128 * j + a(128), o_gv + 256 * j + a(256),
        o_rv + 256 * j + a(256), o_rg + 256 * j + a(256),
        o_gg + 256 * j + a(256), [o_gb + j], [o_ga + j], [o_ga + j]])
    convw = np.concatenate([inp["ab_conv_q"][0][:, 128 * j:128 * j + 128], inp["ab_conv_k"][0][:, 128 * j:128 * j + 128],
                            inp["ab_conv_v"][0][:, 256 * j:256 * j + 256]], axis=1).T
    wo = inp["ab_w_out"][0]
    return {
        "xT": xT_b,
        "w": np.ascontiguousarray(w[:, cols]),
        "cs": cs_tab,
        "convw": np.ascontiguousarray(convw),
        "rc": ret_consts(j),
        "gnr": np.ascontiguousarray(inp["ab_ret_gn_w"][0][256 * j:256 * j + 256].reshape(1, 256)),
        "gng": np.ascontiguousarray(inp["ab_gdn_norm_w"][0].reshape(1, 256)),
        "alog": np.ascontiguousarray(inp["ab_A_log"][0][j:j + 1].reshape(1, 1)),
        "dtb": np.ascontiguousarray(inp["ab_dt_bias"][0][j:j + 1].reshape(1, 1)),
        "wout": np.ascontiguousarray(np.concatenate([wo[256 * j:256 * j + 256], wo[1024 + 256 * j:1024 + 256 * j + 256]], 0)),
    }


_CACHE = {}


def _prog(name, builder):
    if name not in _CACHE:
        _CACHE[name] = builder()
    return _CACHE[name]


def _run(nc, maps):
    return run_bass_kernel_spmd(nc, maps, core_ids=list(range(8))).results


def kernel(**inputs):
    inp = {k: np.asarray(v, dtype=np.float32) for k, v in inputs.items()}
    x = inp["x"]
    xT = []
    for b in range(NB):
        t = np.zeros((D, R), np.float32)
        t[:, PADF:PADF + NMETA] = inp["meta_tokens"].T
        t[:, PADF + NMETA:PADF + NMETA + SEQ] = x[b].T
        xT.append(t)
    cs = rope_tables()

    def lnp(i):
        return np.ascontiguousarray(np.stack([inp["ln1_w"][i], inp["ln1_b"][i], inp["ln2_w"][i], inp["ln2_b"][i]]).astype(np.float32))

    def mlp_launch(layer, resT, parts):
        maps = []
        for c in range(8):
            b, q = c // 4, c % 4
            m = {"res": np.ascontiguousarray(resT[b][:, q * TQ:(q + 1) * TQ]), "w1": inp["mlp_w1"][layer], "w2": inp["mlp_w2"][layer], "lnp": lnp(layer)}
            for j in range(4):
                m["part%d" % j] = np.ascontiguousarray(parts[b * 4 + j]["part"][q])
            maps.append(m)
        r = _run(_prog("mlp", lambda: build_mlp(4)), maps)
        return [np.concatenate([r[b * 4 + q]["hout"] for q in range(4)], axis=1) for b in range(NB)]

    p0 = _run(_prog("ab", build_ab), [ab_inputs(inp, xT[c // 4], c % 4, cs) for c in range(8)])
    h1T = mlp_launch(0, xT, p0)
    p1 = _run(_prog("ssd", build_ssd), [ssd_inputs(inp, h1T[c // 4], c % 4) for c in range(8)])
    h2T = mlp_launch(1, h1T, p1)
    out = np.stack([np.ascontiguousarray(h2T[b][:, PADF + NMETA:PADF + NMETA + SEQ].T) for b in range(NB)], 0)
    return out.astype(np.float32)
```
